# Optimizing a Trainium2 kernel written in Bass

```python
import math
import jax, jax.numpy as jnp
from jax import lax
import numpy as np

D_MODEL = 1024
BATCH = 32
SEQ = 256
DEPTH = 4
DEC_BATCH = 8
DEC_SEQ = 4096
PAST_LEN = 256

GRID_W = 64
N_MIXERS = 2
N_LRU_LAYERS = (DEPTH + 1) // 2
N_SG_LAYERS = DEPTH // 2
LRU_WIDTH = (4 * D_MODEL // 3) // 128 * 128
LRU_HEADS = LRU_WIDTH // 128
LRU_BLOCK = LRU_WIDTH // LRU_HEADS
LRU_C = 8.0
CONV_W = 4
CHUNK = 128
SG_WIDTH = 2 * D_MODEL
SG_GROUPS = 8
SG_GROUP_DIM = SG_WIDTH // SG_GROUPS
N_EXPERTS = 16
EC_FACTOR = 2
EXPERT_FF = D_MODEL
RMS_EPS = 1e-6
POS_BASE = 10000.0

kernel_name = 'hybrid_rglru_sgmlp_ec_diffusion_step'


def _rmsnorm(x, g):
    xf = x.astype(jnp.float32)
    y = xf * lax.rsqrt(jnp.mean(xf * xf, axis=-1, keepdims=True) + RMS_EPS)
    return (y * g.astype(jnp.float32)).astype(x.dtype)


def _grid_pos_embed(n_tokens, dtype):
    rows = n_tokens // GRID_W
    row = jnp.repeat(jnp.arange(rows, dtype=jnp.float32), GRID_W)
    col = jnp.tile(jnp.arange(GRID_W, dtype=jnp.float32), rows)
    q = D_MODEL // 4
    freq = jnp.exp(-math.log(POS_BASE) * jnp.arange(q, dtype=jnp.float32) / q)
    ang_r = row[:, None] * freq
    ang_c = col[:, None] * freq
    emb = jnp.concatenate([jnp.sin(ang_r), jnp.cos(ang_r), jnp.sin(ang_c), jnp.cos(ang_c)], axis=-1)
    return emb.astype(dtype)


def _centred_depthwise_conv(x, w, b):
    t = x.shape[1]
    left = CONV_W // 2
    xp = jnp.pad(x, ((0, 0), (left, CONV_W - 1 - left), (0, 0)))
    return sum(xp[:, k:k + t] * w[k] for k in range(CONV_W)) + b


def _linear_scan(a, u, h0, reverse):
    if reverse:
        a = jnp.flip(a, axis=1)
        u = jnp.flip(u, axis=1)

    def combine(lhs, rhs):
        a_l, u_l = lhs
        a_r, u_r = rhs
        return a_l * a_r, a_r * u_l + u_r

    a_cum, u_cum = lax.associative_scan(combine, (a, u), axis=1)
    h = u_cum + a_cum * h0[:, None, :]
    final = h[:, -1]
    if reverse:
        h = jnp.flip(h, axis=1)
    return h, final


def _rglru_mixer(h, h0, w_in, conv_w, conv_b, w_a, b_a, w_x, b_x, lam, w_out):
    bsz, t, _ = h.shape
    proj = h @ w_in
    gate = jax.nn.gelu(proj[..., :LRU_WIDTH])
    xb = _centred_depthwise_conv(proj[..., LRU_WIDTH:], conv_w, conv_b)
    xh = xb.reshape(bsz, t, LRU_HEADS, LRU_BLOCK)
    hs, finals = [], []
    for d in range(2):
        r = jax.nn.sigmoid(jnp.einsum('bthi,hij->bthj', xh, w_a[d]).reshape(bsz, t, LRU_WIDTH) + b_a[d])
        i = jax.nn.sigmoid(jnp.einsum('bthi,hij->bthj', xh, w_x[d]).reshape(bsz, t, LRU_WIDTH) + b_x[d])
        log_a = -LRU_C * r.astype(jnp.float32) * jax.nn.softplus(-lam[d].astype(jnp.float32))
        a = jnp.exp(log_a)
        u = jnp.sqrt(-jnp.expm1(2.0 * log_a)) * (i * xb).astype(jnp.float32)
        h_d, fin = _linear_scan(a, u, h0[:, d].astype(jnp.float32), reverse=(d == 1))
        hs.append(h_d)
        finals.append(fin)
    y = ((hs[0] + hs[1]).astype(h.dtype) * gate) @ w_out
    return y, jnp.stack(finals, axis=1)


def _sgu_mixer(h, w_in, norm_g, w_s, b_s, w_out):
    bsz, t, _ = h.shape
    proj = jax.nn.gelu(h @ w_in)
    u = proj[..., :SG_WIDTH]
    v = _rmsnorm(proj[..., SG_WIDTH:], norm_g).reshape(bsz, t // CHUNK, CHUNK, SG_GROUPS, SG_GROUP_DIM)
    sv = jnp.einsum('gpq,bnqgc->bnpgc', w_s, v) + b_s.T[None, None, :, :, None]
    return (u * sv.reshape(bsz, t, SG_WIDTH)) @ w_out


def _ec_moe(x, router, w_gate, w_up, w_down):
    bsz, t, d = x.shape
    xf = x.reshape(bsz * t, d)
    cap = EC_FACTOR * (bsz * t) // N_EXPERTS
    aff = jax.nn.softmax(xf.astype(jnp.float32) @ router.astype(jnp.float32), axis=-1)
    g, idx = lax.top_k(aff.T, cap)
    xs = xf[idx]
    hh = jax.nn.silu(jnp.einsum('ecd,edf->ecf', xs, w_gate)) * jnp.einsum('ecd,edf->ecf', xs, w_up)
    out = jnp.einsum('ecf,efd->ecd', hh, w_down) * g[..., None].astype(x.dtype)
    y = jnp.zeros_like(xf).at[idx.reshape(-1)].add(out.reshape(-1, d))
    return y.reshape(bsz, t, d)


def _trunk(x, cond, h0, p):
    finals = []
    sc = jax.nn.silu(cond)
    for l in range(DEPTH):
        mod = (sc @ p['w_mod'][l] + p['b_mod'][l])[:, None, :]
        sh1, s1, g1, sh2, s2, g2 = jnp.split(mod, 6, axis=-1)
        hn = _rmsnorm(x, p['norm1_g'][l]) * (1 + s1) + sh1
        j = l // N_MIXERS
        if l % N_MIXERS == 0:
            y, fin = _rglru_mixer(hn, h0[:, j], p['lru_w_in'][j], p['lru_conv_w'][j], p['lru_conv_b'][j],
                                  p['lru_w_a'][j], p['lru_b_a'][j], p['lru_w_x'][j], p['lru_b_x'][j],
                                  p['lru_lam'][j], p['lru_w_out'][j])
            finals.append(fin)
        else:
            y = _sgu_mixer(hn, p['sg_w_in'][j], p['sg_norm_g'][j], p['sg_w_s'][j], p['sg_b_s'][j],
                           p['sg_w_out'][j])
        x = x + g1 * y
        hn = _rmsnorm(x, p['norm2_g'][l]) * (1 + s2) + sh2
        x = x + g2 * _ec_moe(hn, p['moe_router'][l], p['moe_w_gate'][l], p['moe_w_up'][l], p['moe_w_down'][l])
    return _rmsnorm(x, p['final_norm_g']), jnp.stack(finals, axis=1)


def setup_inputs(seed: int = 0) -> dict:
    key = jax.random.key(seed)
    ks = jax.random.split(key, 28)
    nrm = jax.random.normal
    f = jnp.float32
    d = D_MODEL
    a0 = jax.random.uniform(ks[16], (N_LRU_LAYERS, 2, LRU_WIDTH), f, minval=0.9, maxval=0.999)
    s = a0 ** (1.0 / LRU_C)
    lam = jnp.log(s) - jnp.log1p(-s)
    return {
        'x_prompt': nrm(ks[0], (BATCH, SEQ, d), f),
        'x_sample': nrm(ks[1], (DEC_BATCH, DEC_SEQ, d), f),
        'state_lru': nrm(ks[2], (DEC_BATCH, N_LRU_LAYERS, 2, LRU_WIDTH), f),
        'c': nrm(ks[3], (DEC_BATCH, d), f),
        'c_ctx': nrm(ks[4], (d,), f),
        'norm1_g': 1.0 + 0.01 * nrm(ks[5], (DEPTH, d), f),
        'norm2_g': 1.0 + 0.01 * nrm(ks[6], (DEPTH, d), f),
        'w_mod': nrm(ks[7], (DEPTH, d, 6 * d), f) * (0.25 * d ** -0.5),
        'b_mod': 0.01 * nrm(ks[8], (DEPTH, 6 * d), f),
        'lru_w_in': nrm(ks[9], (N_LRU_LAYERS, d, 2 * LRU_WIDTH), f) * d ** -0.5,
        'lru_conv_w': nrm(ks[10], (N_LRU_LAYERS, CONV_W, LRU_WIDTH), f) * CONV_W ** -0.5,
        'lru_conv_b': 0.01 * nrm(ks[11], (N_LRU_LAYERS, LRU_WIDTH), f),
        'lru_w_a': nrm(ks[12], (N_LRU_LAYERS, 2, LRU_HEADS, LRU_BLOCK, LRU_BLOCK), f) * LRU_BLOCK ** -0.5,
        'lru_b_a': 0.01 * nrm(ks[13], (N_LRU_LAYERS, 2, LRU_WIDTH), f),
        'lru_w_x': nrm(ks[14], (N_LRU_LAYERS, 2, LRU_HEADS, LRU_BLOCK, LRU_BLOCK), f) * LRU_BLOCK ** -0.5,
        'lru_b_x': 0.01 * nrm(ks[15], (N_LRU_LAYERS, 2, LRU_WIDTH), f),
        'lru_lam': lam,
        'lru_w_out': nrm(ks[17], (N_LRU_LAYERS, LRU_WIDTH, d), f) * LRU_WIDTH ** -0.5,
        'sg_w_in': nrm(ks[18], (N_SG_LAYERS, d, 2 * SG_WIDTH), f) * d ** -0.5,
        'sg_norm_g': 1.0 + 0.01 * nrm(ks[19], (N_SG_LAYERS, SG_WIDTH), f),
        'sg_w_s': nrm(ks[20], (N_SG_LAYERS, SG_GROUPS, CHUNK, CHUNK), f) * CHUNK ** -0.5,
        'sg_b_s': 1.0 + 0.1 * nrm(ks[21], (N_SG_LAYERS, SG_GROUPS, CHUNK), f),
        'sg_w_out': nrm(ks[22], (N_SG_LAYERS, SG_WIDTH, d), f) * SG_WIDTH ** -0.5,
        'moe_router': nrm(ks[23], (DEPTH, d, N_EXPERTS), f) * d ** -0.5,
        'moe_w_gate': nrm(ks[24], (DEPTH, N_EXPERTS, d, EXPERT_FF), f) * d ** -0.5,
        'moe_w_up': nrm(ks[25], (DEPTH, N_EXPERTS, d, EXPERT_FF), f) * d ** -0.5,
        'moe_w_down': nrm(ks[26], (DEPTH, N_EXPERTS, EXPERT_FF, d), f) * EXPERT_FF ** -0.5,
        'final_norm_g': 1.0 + 0.01 * nrm(ks[27], (d,), f),
    }


def reference(x_prompt, x_sample, state_lru, c, c_ctx, norm1_g, norm2_g, w_mod, b_mod,
              lru_w_in, lru_conv_w, lru_conv_b, lru_w_a, lru_b_a, lru_w_x, lru_b_x, lru_lam, lru_w_out,
              sg_w_in, sg_norm_g, sg_w_s, sg_b_s, sg_w_out,
              moe_router, moe_w_gate, moe_w_up, moe_w_down, final_norm_g):
    p = {
        'norm1_g': norm1_g, 'norm2_g': norm2_g, 'w_mod': w_mod, 'b_mod': b_mod,
        'lru_w_in': lru_w_in, 'lru_conv_w': lru_conv_w, 'lru_conv_b': lru_conv_b,
        'lru_w_a': lru_w_a, 'lru_b_a': lru_b_a, 'lru_w_x': lru_w_x, 'lru_b_x': lru_b_x,
        'lru_lam': lru_lam, 'lru_w_out': lru_w_out,
        'sg_w_in': sg_w_in, 'sg_norm_g': sg_norm_g, 'sg_w_s': sg_w_s, 'sg_b_s': sg_b_s, 'sg_w_out': sg_w_out,
        'moe_router': moe_router, 'moe_w_gate': moe_w_gate, 'moe_w_up': moe_w_up, 'moe_w_down': moe_w_down,
        'final_norm_g': final_norm_g,
    }
    h0_ctx = jnp.zeros((x_prompt.shape[0], N_LRU_LAYERS, 2, LRU_WIDTH), jnp.float32)
    y_prompt, new_state_lru = _trunk(x_prompt, c_ctx[None, :], h0_ctx, p)
    xs = x_sample + _grid_pos_embed(x_sample.shape[1], x_sample.dtype)[None]
    y_sample, _ = _trunk(xs, c, state_lru, p)
    return (y_prompt, y_sample, new_state_lru)
```

```python
import numpy as np
from contextlib import ExitStack
import concourse.bass as bass
import concourse.mybir as mybir
from concourse.bass_utils import run_bass_kernel_spmd

F32 = mybir.dt.float32
BF16 = mybir.dt.bfloat16
I32 = mybir.dt.int32
AF = mybir.ActivationFunctionType
ALU = mybir.AluOpType

NCORES = 8
D = 1024
T = 5120
TS = 4096
TP = 1024
NT = T // 128
DEPTH = 4
LW = 1280
NCH = 10
NE = 16
CS = 1024
CP = 256
CSL = CS + CP
NSL = CSL // 128
EPS = 1e-6
SKIP_MOE = False
MOE_STOP = 4


class KB:
    def __init__(self, nc):
        self.nc = nc
        self.E = {'pe': nc.tensor, 'dve': nc.vector, 'act': nc.scalar, 'pool': nc.gpsimd, 'sp': nc.sync}
        self.sems, self.cnt = {}, {}
        self.waited = {e: {} for e in self.E}
        self.lastw, self.readers = {}, {}
        self.slots, self.rr = {}, {}
        self.root = ExitStack()
        self.uid = 0
        for e in ['pe', 'dve', 'act', 'pool', 'cc']:
            self.sems[e] = self.root.enter_context(nc.semaphore('s_' + e))
            self.cnt[e] = 0
        for q, n in [('sp', 24), ('pool', 12), ('act', 6)]:
            self.slots[q] = []
            for i in range(n):
                key = 'd_%s%d' % (q, i)
                self.sems[key] = self.root.enter_context(nc.semaphore(key))
                self.cnt[key] = 0
                self.slots[q].append(key)
            self.rr[q] = 0

    def _deps(self, r, w):
        deps = set()
        for x in r:
            t = self.lastw.get(x)
            if t:
                deps.add(t)
        for x in w:
            t = self.lastw.get(x)
            if t:
                deps.add(t)
            deps.update(self.readers.get(x, {}).items())
        return deps

    def _wait(self, eng, deps):
        for k, v in sorted(deps):
            if k == eng and eng == 'pe':
                continue
            if self.waited[eng].get(k, 0) >= v:
                continue
            self.E[eng].wait_ge(self.sems[k], v)
            self.waited[eng][k] = v

    def _record(self, tok, r, w):
        for x in r:
            d = self.readers.setdefault(x, {})
            if d.get(tok[0], 0) < tok[1]:
                d[tok[0]] = tok[1]
        for x in w:
            self.lastw[x] = tok
            self.readers[x] = {}

    def op(self, eng, fn, r=(), w=()):
        self._wait(eng, self._deps(r, w))
        ins = fn(self.E[eng])
        self.cnt[eng] += 1
        ins.then_inc(self.sems[eng], 1)
        self._record((eng, self.cnt[eng]), r, w)

    def dma(self, q, fn, r=(), w=()):
        sl = self.slots[q]
        key = sl[self.rr[q] % len(sl)]
        self.rr[q] += 1
        deps = self._deps(r, w)
        if self.cnt[key] > 0:
            deps.add((key, self.cnt[key]))
        self._wait(q, deps)
        ins = fn(self.E[q])
        self.cnt[key] += 16
        ins.then_inc(self.sems[key], 16)
        self._record((key, self.cnt[key]), r, w)

    def cc(self, fn, r=(), w=()):
        self._wait('pool', self._deps(r, w))
        ins = fn(self.E['pool'])
        self.cnt['cc'] += 1
        ins.then_inc(self.sems['cc'])
        self._record(('cc', self.cnt['cc']), r, w)

    def barrier(self):
        alltok = set((k, c) for k, c in self.cnt.items() if c > 0)
        for e in ['pe', 'dve', 'act', 'pool', 'sp']:
            self._wait(e, alltok)
        self.lastw.clear()
        self.readers.clear()

    def finish(self):
        alltok = set((k, c) for k, c in self.cnt.items() if c > 0)
        self._wait('sp', alltok)

    def sb(self, es, shape, dt, name='t'):
        self.uid += 1
        return es.enter_context(self.nc.sbuf_tensor('%s_%d' % (name, self.uid), list(shape), dt))

    def ps(self, es, shape, dt, name='p'):
        self.uid += 1
        return es.enter_context(self.nc.psum_tensor('%s_%d' % (name, self.uid), list(shape), dt))


def build_program(n_layers=DEPTH, stage="full"):
    nc = bass.Bass("TRN2", target_bir_lowering=False)
    kb = KB(nc)
    dt_in = lambda name, shape, dt=F32: nc.dram_tensor(name, list(shape), dt, kind="ExternalInput").ap()
    dt_out = lambda name, shape, dt=F32: nc.dram_tensor(name, list(shape), dt, kind="ExternalOutput").ap()

    x_in = dt_in("x_in", [T, D])
    pos_in = dt_in("pos_in", [TS, D])
    cond_in = dt_in("cond_in", [2, D])
    h0_in = dt_in("h0_in", [2, 2, LW])
    norm1_g = dt_in("norm1_g", [DEPTH, D]); norm2_g = dt_in("norm2_g", [DEPTH, D])
    w_mod = dt_in("w_mod", [DEPTH, D, 6 * D]); b_mod = dt_in("b_mod", [DEPTH, 6 * D])
    lru_w_in = dt_in("lru_w_in", [2, D, 2 * LW])
    lru_conv_w = dt_in("lru_conv_w", [2, 4, LW]); lru_conv_b = dt_in("lru_conv_b", [2, LW])
    lru_w_a = dt_in("lru_w_a", [2, 2, NCH, 128, 128]); lru_b_a = dt_in("lru_b_a", [2, 2, LW])
    lru_w_x = dt_in("lru_w_x", [2, 2, NCH, 128, 128]); lru_b_x = dt_in("lru_b_x", [2, 2, LW])
    lru_lam = dt_in("lru_lam", [2, 2, LW]); lru_w_out = dt_in("lru_w_out", [2, LW, D])
    sg_w_in = dt_in("sg_w_in", [2, D, 4096]); sg_norm_g = dt_in("sg_norm_g", [2, 2048])
    sg_w_sT = dt_in("sg_w_sT", [2, 8, 128, 128]); sg_b_s = dt_in("sg_b_s", [2, 8, 128])
    sg_w_out = dt_in("sg_w_out", [2, 2048, D])
    moe_router = dt_in("moe_router", [DEPTH, D, NE])
    moe_w_gate = dt_in("moe_w_gate", [DEPTH, NE, D, D]); moe_w_up = dt_in("moe_w_up", [DEPTH, NE, D, D])
    moe_w_down = dt_in("moe_w_down", [DEPTH, NE, D, D])
    final_norm_g = dt_in("final_norm_g", [1, D])

    y_out = dt_out("y_out", [T, D])
    st_out = dt_out("st_out", [4, 2, 2, LW])

    xres = nc.dram_tensor("xres", [T + 128, D], F32).ap()
    modv = nc.dram_tensor("modv", [DEPTH, 2, 6 * D], F32).ap()
    zbuf = nc.dram_tensor("zbuf", [NCH, 128, T], BF16).ap()
    hn2d = nc.dram_tensor("hn2d", [T + 128, D], BF16).ap()
    zbuf2 = nc.dram_tensor("zbuf2", [16, 128, T], BF16).ap()
    thr_d = nc.dram_tensor("thr_d", [2, NE], F32).ap()
    aff_loc_t = nc.dram_tensor("aff_loc", [NE, T], F32)
    aff_all_t = nc.dram_tensor("aff_all", [NCORES * NE, T], F32)

    top = ExitStack()
    ident_f = kb.sb(top, [128, 128], F32, 'identf')
    ident_b = kb.sb(top, [128, 128], BF16, 'identb')
    tri_b = kb.sb(top, [128, 128], BF16, 'trib')
    ones_b = kb.sb(top, [128, 128], BF16, 'onesb')
    iota_row = kb.sb(top, [128, CS], F32, 'iotar')
    iota_col = kb.sb(top, [128, 1], F32, 'iotac')
    iota_row1 = kb.sb(top, [128, CS], F32, 'iotar1')
    kb.op('pool', lambda e: e.iota(iota_row[:], [[1, CS]], base=0, channel_multiplier=0,
                                   allow_small_or_imprecise_dtypes=True), w=['iotar'])
    kb.op('pool', lambda e: e.iota(iota_row1[:], [[1, CS]], base=1, channel_multiplier=0,
                                   allow_small_or_imprecise_dtypes=True), w=['iotar1'])
    kb.op('pool', lambda e: e.iota(iota_col[:], [[0, 1]], base=0, channel_multiplier=1,
                                   allow_small_or_imprecise_dtypes=True), w=['iotac'])
    kb.op('dve', lambda e: e.tensor_scalar(out=ident_f[:], in0=iota_row[:, 0:128], scalar1=iota_col[:, 0:1],
                                           scalar2=None, op0=ALU.is_equal), r=['iotar', 'iotac'], w=['identf'])
    kb.op('dve', lambda e: e.tensor_copy(out=ident_b[:], in_=ident_f[:]), r=['identf'], w=['identb'])
    kb.op('dve', lambda e: e.tensor_scalar(out=tri_b[:], in0=iota_row[:, 0:128], scalar1=iota_col[:, 0:1],
                                           scalar2=None, op0=ALU.is_ge), r=['iotar', 'iotac'], w=['trib'])
    kb.op('dve', lambda e: e.memset(ones_b[:], 1.0), w=['onesb'])
    with ExitStack() as es0:
        zt0 = kb.sb(es0, [128, D], BF16, 'zt0')
        kb.op('dve', lambda e: e.memset(zt0[:], 0.0), w=['zt0'])
        kb.dma('sp', lambda e: e.dma_start(out=hn2d[T:T + 128, :], in_=zt0[:]), r=['zt0'], w=['hn2dz'])
        kb.barrier()

    with ExitStack() as es:
        c2 = kb.sb(es, [128, 2, 8], F32, 'c2')
        sig = kb.sb(es, [128, 2, 8], F32, 'sig')
        scT = kb.sb(es, [128, 8, 2], BF16, 'scT')
        wt = [kb.sb(es, [128, 8, 512], BF16, 'wmod') for _ in range(3)]
        bm = kb.sb(es, [2, 6 * D], F32, 'bm')
        mo = kb.sb(es, [2, 6 * D], F32, 'mo')
        pm = [kb.ps(es, [128, 512], F32, 'pm') for _ in range(2)]
        for ci in range(2):
            kb.dma('sp', lambda e: e.dma_start(out=c2[:, ci, :], in_=cond_in[ci].rearrange("(k p) -> p k", p=128),
                                               allow_slow_non_contiguous=True), w=['c2'])
        kb.op('act', lambda e: e.activation(out=sig[:], in_=c2[:], func=AF.Sigmoid), r=['c2'], w=['sig'])
        kb.op('dve', lambda e: e.tensor_tensor(out=scT[:].rearrange("p k c -> p c k"), in0=c2[:], in1=sig[:], op=ALU.mult), r=['c2', 'sig'], w=['scT'])
        q = 0
        for l in range(n_layers):
            kb.dma('sp', lambda e: e.dma_start(out=bm[:], in_=b_mod[l:l + 1, :].partition_broadcast(2)), w=['bm'])
            for cb in range(12):
                wtile = wt[q % 3]; wn = ('wmod', q % 3); pt = pm[q % 2]; pn = ('pm', q % 2)
                kb.dma('pool', lambda e: e.dma_start(out=wtile[:], in_=w_mod[l, :, cb * 512:(cb + 1) * 512]
                                                     .rearrange("(k p) f -> p k f", p=128)), w=[wn])
                for k in range(8):
                    kb.op('pe', lambda e: e.matmul(pt[0:2, :], lhsT=scT[:, k, :], rhs=wtile[:, k, :],
                                                   start=(k == 0), stop=(k == 7)), r=['scT', wn], w=[pn])
                kb.op('dve', lambda e: e.tensor_tensor(out=mo[:, cb * 512:(cb + 1) * 512], in0=pt[0:2, :],
                                                       in1=bm[:, cb * 512:(cb + 1) * 512], op=ALU.add),
                      r=[pn, 'bm'], w=['mo'])
                q += 1
            kb.dma('sp', lambda e: e.dma_start(out=modv[l], in_=mo[:]), r=['mo'], w=[('modv', l)])
        kb.barrier()

    def load_mod_bc(es, l, which, ng_ap, cnds=(0, 1), want='ABG'):
        o = 0 if which == 1 else 3
        res = {}
        for cnd in cnds:
            A = B = G = None
            an, bn, gn = ('A', cnd), ('B', cnd), ('G', cnd)
            if 'B' in want:
                B = kb.sb(es, [128, D], F32, 'B')
                kb.dma('sp', lambda e: e.dma_start(out=B[:], in_=modv[l, cnd:cnd + 1, (o + 0) * D:(o + 1) * D].partition_broadcast(128)),
                       r=[('modv', l)], w=[bn])
            if 'A' in want:
                A = kb.sb(es, [128, D], F32, 'A')
                with ExitStack() as est:
                    ngt = kb.sb(est, [128, D], F32, 'ngt')
                    kb.dma('sp', lambda e: e.dma_start(out=ngt[:], in_=ng_ap.partition_broadcast(128)), w=['ngt'])
                    kb.dma('sp', lambda e: e.dma_start(out=A[:], in_=modv[l, cnd:cnd + 1, (o + 1) * D:(o + 2) * D].partition_broadcast(128)),
                           r=[('modv', l)], w=[an])
                    kb.op('dve', lambda e: e.scalar_tensor_tensor(out=A[:], in0=A[:], scalar=1.0, in1=ngt[:],
                                                                  op0=ALU.add, op1=ALU.mult), r=[an, 'ngt'], w=[an])
                    kb.barrier()
            if 'G' in want:
                G = kb.sb(es, [128, D], F32, 'G')
                kb.dma('sp', lambda e: e.dma_start(out=G[:], in_=modv[l, cnd:cnd + 1, (o + 2) * D:(o + 3) * D].partition_broadcast(128)),
                       r=[('modv', l)], w=[gn])
            res[cnd] = (A, B, G)
        return res

    def load_x_tile(l_first, i, xt, xn, pt=None, pn=None):
        if l_first:
            kb.dma('sp', lambda e: e.dma_start(out=xt[:], in_=x_in[i * 128:(i + 1) * 128, :]), w=[xn])
            if i < TS // 128:
                kb.dma('sp', lambda e: e.dma_start(out=pt[:], in_=pos_in[i * 128:(i + 1) * 128, :]), w=[pn])
                kb.op('dve', lambda e: e.tensor_tensor(out=xt[:], in0=xt[:], in1=pt[:], op=ALU.add), r=[xn, pn], w=[xn])
        else:
            kb.dma('sp', lambda e: e.dma_start(out=xt[:], in_=xres[i * 128:(i + 1) * 128, :]), r=['xres'], w=[xn])

    def norm_mod(xt, xn, A, an, B, bn, hn, hnn, scr, st, sn):
        kb.op('act', lambda e: e.activation(out=scr[:], in_=xt[:], func=AF.Square, accum_out=st[:, 0:1]),
              r=[xn], w=['scr', sn])
        kb.op('dve', lambda e: e.tensor_scalar(out=st[:, 1:2], in0=st[:, 0:1], scalar1=1.0 / D, scalar2=EPS,
                                               op0=ALU.mult, op1=ALU.add), r=[sn], w=[sn])
        kb.op('act', lambda e: e.activation(out=st[:, 2:3], in_=st[:, 1:2], func=AF.Sqrt), r=[sn], w=[sn])
        kb.op('dve', lambda e: e.reciprocal(out=st[:, 3:4], in_=st[:, 2:3]), r=[sn], w=[sn])
        kb.op('dve', lambda e: e.scalar_tensor_tensor(out=scr[:], in0=xt[:], scalar=st[:, 3:4], in1=A[:],
                                                      op0=ALU.mult, op1=ALU.mult), r=[xn, sn, an], w=['scr'])
        kb.op('dve', lambda e: e.tensor_tensor(out=hn[:], in0=scr[:], in1=B[:], op=ALU.add), r=['scr', bn], w=[hnn])

    def lru_phase(l, j):
        first = (l == 0)
        with ExitStack() as es:
            cw = kb.sb(es, [128, NCH, 4], F32, 'cw'); cb_ = kb.sb(es, [128, NCH], F32, 'cb')
            ba = kb.sb(es, [128, 2, NCH], F32, 'ba'); bx = kb.sb(es, [128, 2, NCH], F32, 'bx')
            lam = kb.sb(es, [128, 2, NCH], F32, 'lam'); sca = kb.sb(es, [128, 2, NCH], F32, 'sca')
            h0t = kb.sb(es, [128, 2, NCH], F32, 'h0t')
            fin = kb.sb(es, [128, 4, 2, NCH], F32, 'fin')
            wg = kb.sb(es, [128, 4 * NCH, 128], BF16, 'wg')
            for kk in range(4):
                kb.dma('sp', lambda e: e.dma_start(out=cw[:, :, kk], in_=lru_conv_w[j, kk].rearrange("(c p) -> p c", p=128),
                                                   allow_slow_non_contiguous=True), w=['cw'])
            kb.dma('sp', lambda e: e.dma_start(out=cb_[:], in_=lru_conv_b[j].rearrange("(c p) -> p c", p=128),
                                               allow_slow_non_contiguous=True), w=['cb'])
            for tname, src, dst in (('ba', lru_b_a, ba), ('bx', lru_b_x, bx), ('lam', lru_lam, lam), ('h0t', h0_in, h0t)):
                for dd in range(2):
                    kb.dma('sp', lambda e: e.dma_start(out=dst[:, dd, :], in_=src[j, dd].rearrange("(c p) -> p c", p=128),
                                                       allow_slow_non_contiguous=True), w=[tname])
            kb.dma('pool', lambda e: e.dma_start(out=wg[:, 0:2 * NCH, :], in_=lru_w_a[j].rearrange("d h i o -> i (d h) o")), w=['wg'])
            kb.dma('pool', lambda e: e.dma_start(out=wg[:, 2 * NCH:4 * NCH, :], in_=lru_w_x[j].rearrange("d h i o -> i (d h) o")), w=['wg'])
            kb.op('act', lambda e: e.activation(out=sca[:], in_=lam[:], func=AF.Exp, scale=-1.0), r=['lam'], w=['sca'])
            kb.op('act', lambda e: e.activation(out=sca[:], in_=sca[:], func=AF.Ln, bias=1.0), r=['sca'], w=['sca'])
            kb.op('dve', lambda e: e.tensor_scalar(out=sca[:], in0=sca[:], scalar1=-8.0, scalar2=None, op0=ALU.mult),
                  r=['sca'], w=['sca'])
            kb.op('dve', lambda e: e.memset(fin[:], 0.0), w=['fin'])

            for grp in range(2):
                nseq, L = (1, TS) if grp == 0 else (4, 256)
                S = nseq * L
                tok0 = 0 if grp == 0 else TS
                an, bn, gn = ('A', grp), ('B', grp), ('G', grp)
                with ExitStack() as es2:
                    A_, B_, _g = load_mod_bc(es2, l, 1, norm1_g[l:l + 1, :], cnds=(grp,), want='AB')[grp]
                    hnT = kb.sb(es2, [128, 8, S], BF16, 'hnT')
                    with ExitStack() as es3:
                        xb_ = [kb.sb(es3, [128, D], F32, 'xt') for _ in range(2)]
                        pb_ = [kb.sb(es3, [128, D], F32, 'pt') for _ in range(2)]
                        scr = kb.sb(es3, [128, D], F32, 'scr')
                        hnb = [kb.sb(es3, [128, D], BF16, 'hn') for _ in range(2)]
                        stt = [kb.sb(es3, [128, 4], F32, 'st') for _ in range(2)]
                        pT = [kb.ps(es3, [128, 8, 128], BF16, 'pT') for _ in range(2)]
                        for ti in range(S // 128):
                            i = tok0 // 128 + ti
                            b = ti % 2
                            load_x_tile(first, i, xb_[b], ('xt', b), pb_[b], ('pt', b))
                            norm_mod(xb_[b], ('xt', b), A_, an, B_, bn, hnb[b], ('hn', b), scr, stt[b], ('st', b))
                            for k in range(8):
                                kb.op('pe', lambda e: e.transpose(out=pT[b][:, k, :], in_=hnb[b][:, k * 128:(k + 1) * 128],
                                                                  identity=ident_b[:]), r=[('hn', b), 'identb'], w=[('pT', b)])
                            kb.op('act', lambda e: e.copy(out=hnT[:, :, ti * 128:(ti + 1) * 128], in_=pT[b][:]),
                                  r=[('pT', b)], w=['hnT'])
                        kb.barrier()
                    with ExitStack() as es3:
                        SEG = min(L, 1024)
                        nseg = L // SEG
                        wq = [kb.sb(es3, [128, 8, 2, 128], BF16, 'wq') for _ in range(2)]
                        xraw = kb.sb(es3, [128, nseq, L + 3], F32, 'xraw')
                        XB = kb.sb(es3, [128, nseq, L], F32, 'XB')
                        HF = kb.sb(es3, [128, nseq, L], F32, 'HF')
                        xbf = kb.sb(es3, [128, nseq, L], BF16, 'xbf')
                        GT = kb.sb(es3, [128, nseq, L], BF16, 'GT')
                        ZT = kb.sb(es3, [128, nseq, L], BF16, 'ZT')
                        TA = kb.sb(es3, [128, SEG], F32, 'TA'); TB = kb.sb(es3, [128, SEG], F32, 'TB')
                        TC = kb.sb(es3, [128, SEG], F32, 'TC')
                        stv = kb.sb(es3, [128, 2], F32, 'stv')
                        pp = [kb.ps(es3, [128, 512], F32, 'pp') for _ in range(6)]
                        pq = [0]

                        def nextp():
                            pq[0] += 1
                            return pp[pq[0] % 6], ('pp', pq[0] % 6)
                        kb.op('dve', lambda e: e.memset(xraw[:], 0.0), w=['xraw'])
                        gw = 512 // L if L < 512 else 1
                        for c in range(NCH):
                            wb = wq[c % 2]; wn = ('wq', c % 2)
                            kb.dma('pool', lambda e: e.dma_start(out=wb[:, :, 0, :], in_=lru_w_in[j, :, LW + c * 128:LW + (c + 1) * 128]
                                                                 .rearrange("(k p) f -> p k f", p=128)), w=[wn])
                            kb.dma('pool', lambda e: e.dma_start(out=wb[:, :, 1, :], in_=lru_w_in[j, :, c * 128:(c + 1) * 128]
                                                                 .rearrange("(k p) f -> p k f", p=128)), w=[wn])
                            for n in range(S // 512):
                                pt_, pn_ = nextp()
                                for k in range(8):
                                    kb.op('pe', lambda e: e.matmul(pt_[:], lhsT=wb[:, k, 0, :], rhs=hnT[:, k, n * 512:(n + 1) * 512],
                                                                   start=(k == 0), stop=(k == 7)), r=[wn, 'hnT'], w=[pn_])
                                if L >= 512:
                                    s_, t_ = (n * 512) // L, (n * 512) % L
                                    kb.op('act', lambda e: e.copy(out=xraw[:, s_, 2 + t_:2 + t_ + 512], in_=pt_[:]), r=[pn_], w=['xraw'])
                                else:
                                    kb.op('act', lambda e: e.copy(out=xraw[:, n * gw:(n + 1) * gw, 2:2 + L],
                                                                  in_=pt_[:].rearrange("p (s t) -> p s t", s=gw)), r=[pn_], w=['xraw'])
                                pt2, pn2 = nextp()
                                for k in range(8):
                                    kb.op('pe', lambda e: e.matmul(pt2[:], lhsT=wb[:, k, 1, :], rhs=hnT[:, k, n * 512:(n + 1) * 512],
                                                                   start=(k == 0), stop=(k == 7)), r=[wn, 'hnT'], w=[pn2])
                                if L >= 512:
                                    s_, t_ = (n * 512) // L, (n * 512) % L
                                    kb.op('act', lambda e: e.activation(out=GT[:, s_, t_:t_ + 512], in_=pt2[:], func=AF.Gelu_apprx_tanh),
                                          r=[pn2], w=['GT'])
                                else:
                                    kb.op('act', lambda e: e.activation(out=GT[:, n * gw:(n + 1) * gw, :],
                                                                        in_=pt2[:].rearrange("p (s t) -> p s t", s=gw),
                                                                        func=AF.Gelu_apprx_tanh), r=[pn2], w=['GT'])
                            kb.op('dve', lambda e: e.tensor_scalar(out=XB[:], in0=xraw[:, :, 0:L], scalar1=cw[:, c, 0:1],
                                                                   scalar2=cb_[:, c:c + 1], op0=ALU.mult, op1=ALU.add),
                                  r=['xraw', 'cw', 'cb'], w=['XB'])
                            for kk in range(1, 4):
                                kb.op('dve', lambda e: e.scalar_tensor_tensor(out=XB[:], in0=xraw[:, :, kk:kk + L], scalar=cw[:, c, kk:kk + 1],
                                                                              in1=XB[:], op0=ALU.mult, op1=ALU.add),
                                      r=['xraw', 'cw', 'XB'], w=['XB'])
                            kb.op('act', lambda e: e.copy(out=xbf[:], in_=XB[:]), r=['XB'], w=['xbf'])
                            for d in range(2):
                                for s in range(nseq):
                                    segs = list(range(nseg)) if d == 0 else list(range(nseg - 1, -1, -1))
                                    for si, sg in enumerate(segs):
                                        t0 = sg * SEG
                                        for gi in range(SEG // 512 if SEG >= 512 else 1):
                                            w_ = min(512, SEG)
                                            a0 = t0 + gi * w_
                                            for typ, dstT, bias_t, bname in ((0, TA, ba, 'ba'), (1, TB, bx, 'bx')):
                                                pt_, pn_ = nextp()
                                                kb.op('pe', lambda e: e.matmul(pt_[:, 0:w_], lhsT=wg[:, (typ * 2 + d) * NCH + c, :],
                                                                               rhs=xbf[:, s, a0:a0 + w_], start=True, stop=True),
                                                      r=['wg', 'xbf'], w=[pn_])
                                                kb.op('act', lambda e: e.activation(out=dstT[:, gi * w_:(gi + 1) * w_], in_=pt_[:, 0:w_],
                                                                                    func=AF.Sigmoid, bias=bias_t[:, d, c:c + 1]),
                                                      r=[pn_, bname], w=['TA' if typ == 0 else 'TB'])
                                        kb.op('act', lambda e: e.activation(out=TA[:], in_=TA[:], func=AF.Exp, scale=sca[:, d, c:c + 1]),
                                              r=['TA', 'sca'], w=['TA'])
                                        kb.op('act', lambda e: e.activation(out=TC[:], in_=TA[:], func=AF.Square), r=['TA'], w=['TC'])
                                        kb.op('act', lambda e: e.activation(out=TC[:], in_=TC[:], func=AF.Sqrt, scale=-1.0, bias=1.0),
                                              r=['TC'], w=['TC'])
                                        kb.op('dve', lambda e: e.tensor_tensor(out=TB[:], in0=TB[:], in1=XB[:, s, t0:t0 + SEG], op=ALU.mult),
                                              r=['TB', 'XB'], w=['TB'])
                                        kb.op('dve', lambda e: e.tensor_tensor(out=TB[:], in0=TB[:], in1=TC[:], op=ALU.mult),
                                              r=['TB', 'TC'], w=['TB'])
                                        init = h0t[:, d, c:c + 1] if (si == 0 and grp == 0) else (stv[:, d:d + 1] if si > 0 else 0.0)
                                        rr_ = ['TA', 'TB'] + (['h0t'] if (si == 0 and grp == 0) else (['stv'] if si > 0 else []))
                                        if d == 0:
                                            kb.op('dve', lambda e: e.tensor_tensor_scan(out=HF[:, s, t0:t0 + SEG], data0=TA[:], data1=TB[:],
                                                                                        initial=init, op0=ALU.mult, op1=ALU.add),
                                                  r=rr_, w=['HF'])
                                            if si < nseg - 1:
                                                kb.op('act', lambda e: e.copy(out=stv[:, 0:1], in_=HF[:, s, t0 + SEG - 1:t0 + SEG]), r=['HF'], w=['stv'])
                                            elif grp == 1:
                                                kb.op('act', lambda e: e.copy(out=fin[:, s, 0, c:c + 1], in_=HF[:, s, L - 1:L]), r=['HF'], w=['fin'])
                                        else:
                                            kb.op('dve', lambda e: e.tensor_tensor_scan(out=TC[:, ::-1], data0=TA[:, ::-1], data1=TB[:, ::-1],
                                                                                        initial=init, op0=ALU.mult, op1=ALU.add),
                                                  r=rr_ + ['TC'], w=['TC'])
                                            if si < nseg - 1:
                                                kb.op('act', lambda e: e.copy(out=stv[:, 1:2], in_=TC[:, 0:1]), r=['TC'], w=['stv'])
                                            elif grp == 1:
                                                kb.op('act', lambda e: e.copy(out=fin[:, s, 1, c:c + 1], in_=TC[:, 0:1]), r=['TC'], w=['fin'])
                                            kb.op('dve', lambda e: e.tensor_tensor(out=HF[:, s, t0:t0 + SEG], in0=HF[:, s, t0:t0 + SEG], in1=TC[:], op=ALU.add),
                                                  r=['HF', 'TC'], w=['HF'])
                            kb.op('dve', lambda e: e.tensor_tensor(out=ZT[:], in0=HF[:], in1=GT[:], op=ALU.mult), r=['HF', 'GT'], w=['ZT'])
                            kb.dma('sp', lambda e: e.dma_start(out=zbuf[c, :, tok0:tok0 + S], in_=ZT[:].rearrange("p s t -> p (s t)")),
                                   r=['ZT'], w=['zbuf'])
                        kb.barrier()
            for ss in range(4):
                for dd in range(2):
                    kb.dma('sp', lambda e: e.dma_start(out=st_out[ss, j, dd, :].rearrange("(c p) -> p c", p=128), in_=fin[:, ss, dd, :],
                                                       allow_slow_non_contiguous=True), r=['fin'], w=[('st_out', j, ss, dd)])
            with ExitStack() as es3:
                mods = load_mod_bc(es3, l, 1, norm1_g[l:l + 1, :], want='G')
                wo = kb.sb(es3, [128, NCH, D], BF16, 'wo')
                kb.dma('pool', lambda e: e.dma_start(out=wo[:], in_=lru_w_out[j].rearrange("(c p) f -> p c f", p=128)), w=['wo'])
                zt = [kb.sb(es3, [128, NCH, 512], BF16, 'zt') for _ in range(2)]
                xb_ = [kb.sb(es3, [128, D], F32, 'xt') for _ in range(3)]
                pb_ = [kb.sb(es3, [128, D], F32, 'pt') for _ in range(2)]
                tmp = [kb.sb(es3, [128, D], F32, 'tmp') for _ in range(2)]
                po = [kb.ps(es3, [128, 2, 512], F32, 'po') for _ in range(3)]
                for g in range(T // 512):
                    zb = zt[g % 2]; zn = ('zt', g % 2)
                    kb.dma('sp', lambda e: e.dma_start(out=zb[:], in_=zbuf[:, :, g * 512:(g + 1) * 512].rearrange("c p t -> p c t")),
                           r=['zbuf'], w=[zn])
                    grp = 0 if g * 512 < TS else 1
                    G_ = mods[grp][2]; gn = ('G', grp)
                    for q_ in range(4):
                        i = g * 4 + q_
                        xb = xb_[i % 3]; xn = ('xt', i % 3)
                        load_x_tile(first, i, xb, xn, pb_[i % 2], ('pt', i % 2))
                        pt_ = po[i % 3]; pn_ = ('po', i % 3)
                        for hf in range(2):
                            for c in range(NCH):
                                kb.op('pe', lambda e: e.matmul(pt_[:, hf, :], lhsT=zb[:, c, q_ * 128:(q_ + 1) * 128],
                                                               rhs=wo[:, c, hf * 512:(hf + 1) * 512], start=(c == 0), stop=(c == NCH - 1)),
                                      r=[zn, 'wo'], w=[pn_])
                        tm = tmp[i % 2]; tn = ('tmp', i % 2)
                        kb.op('dve', lambda e: e.tensor_tensor(out=tm[:], in0=pt_[:].rearrange("p a b -> p (a b)"), in1=G_[:], op=ALU.mult),
                              r=[pn_, gn], w=[tn])
                        kb.op('dve', lambda e: e.tensor_tensor(out=tm[:], in0=tm[:], in1=xb[:], op=ALU.add), r=[tn, xn], w=[tn])
                        kb.dma('sp', lambda e: e.dma_start(out=xres[i * 128:(i + 1) * 128, :], in_=tm[:]), r=[tn], w=['xres_w'])
                kb.barrier()

    def sgu_phase(l, j):
        with ExitStack() as es:
            win = kb.sb(es, [128, 8, 4096], BF16, 'win')
            for cbk in range(8):
                kb.dma('pool', lambda e: e.dma_start(out=win[:, :, cbk * 512:(cbk + 1) * 512],
                                                     in_=sg_w_in[j, :, cbk * 512:(cbk + 1) * 512].rearrange("(k p) f -> p k f", p=128)), w=['win'])
            wsT = kb.sb(es, [128, 8, 128], BF16, 'wsT')
            kb.dma('pool', lambda e: e.dma_start(out=wsT[:], in_=sg_w_sT[j].rearrange("g q p -> q g p")), w=['wsT'])
            ngv = kb.sb(es, [128, 2048], F32, 'ngv')
            kb.dma('sp', lambda e: e.dma_start(out=ngv[:], in_=sg_norm_g[j:j + 1, :].partition_broadcast(128)), w=['ngv'])
            bs16 = kb.sb(es, [128, 8, 2, 128], F32, 'bs16')
            for cc in range(2):
                kb.dma('sp', lambda e: e.dma_start(out=bs16[:, :, cc, :], in_=sg_b_s[j:j + 1].rearrange("a g p -> a (g p)").partition_broadcast(128)
                                                   .rearrange("q a (g p) -> q (a g) p", g=8)), w=['bs16'])
            bsf = bs16[:].rearrange("q g c p -> q (g c) p")
            xb_ = [kb.sb(es, [128, D], F32, 'xt') for _ in range(2)]
            scr = kb.sb(es, [128, D], F32, 'scr')
            hnb = [kb.sb(es, [128, D], BF16, 'hn') for _ in range(2)]
            stt = [kb.sb(es, [128, 4], F32, 'st') for _ in range(2)]
            hnT = [kb.sb(es, [128, 8, 512], BF16, 'hnT') for _ in range(2)]
            uT = kb.sb(es, [128, 16, 512], BF16, 'uT')
            zT = kb.sb(es, [128, 16, 512], BF16, 'zT')
            vv = kb.sb(es, [128, 2048], F32, 'vv')
            vn = kb.sb(es, [128, 2048], BF16, 'vn')
            svt = kb.sb(es, [128, 4, 128], F32, 'svt')
            sv2 = kb.sb(es, [128, 4], F32, 'sv2')
            pT = [kb.ps(es, [128, 8, 128], BF16, 'pT') for _ in range(2)]
            pm_ = [kb.ps(es, [128, 512], F32, 'pm') for _ in range(4)]
            psp = [kb.ps(es, [128, 4, 128], F32, 'psp') for _ in range(2)]
            pq = [0]
            sq = [0]

            def nextp():
                pq[0] += 1
                return pm_[pq[0] % 4], ('pm', pq[0] % 4)
            for grp in range(2):
                with ExitStack() as es2:
                    A_, B_, _g = load_mod_bc(es2, l, 1, norm1_g[l:l + 1, :], cnds=(grp,), want='AB')[grp]
                    an, bn = ('A', grp), ('B', grp)
                    g0, g1 = (0, TS // 512) if grp == 0 else (TS // 512, T // 512)
                    for g in range(g0, g1):
                        hT = hnT[g % 2]; hTn = ('hnT', g % 2)
                        for q_ in range(4):
                            i = g * 4 + q_
                            b = i % 2
                            load_x_tile(False, i, xb_[b], ('xt', b))
                            norm_mod(xb_[b], ('xt', b), A_, an, B_, bn, hnb[b], ('hn', b), scr, stt[b], ('st', b))
                            for k in range(8):
                                kb.op('pe', lambda e: e.transpose(out=pT[b][:, k, :], in_=hnb[b][:, k * 128:(k + 1) * 128],
                                                                  identity=ident_b[:]), r=[('hn', b), 'identb'], w=[('pT', b)])
                            kb.op('act', lambda e: e.copy(out=hT[:, :, q_ * 128:(q_ + 1) * 128], in_=pT[b][:]), r=[('pT', b)], w=[hTn])
                        for fc in range(16):
                            pt_, pn_ = nextp()
                            for k in range(8):
                                kb.op('pe', lambda e: e.matmul(pt_[:], lhsT=win[:, k, fc * 128:(fc + 1) * 128], rhs=hT[:, k, :],
                                                               start=(k == 0), stop=(k == 7)), r=['win', hTn], w=[pn_])
                            kb.op('act', lambda e: e.activation(out=uT[:, fc, :], in_=pt_[:], func=AF.Gelu_apprx_tanh), r=[pn_], w=['uT'])
                        for q_ in range(4):
                            for cbk in range(4):
                                pt_, pn_ = nextp()
                                for k in range(8):
                                    kb.op('pe', lambda e: e.matmul(pt_[:], lhsT=hT[:, k, q_ * 128:(q_ + 1) * 128],
                                                                   rhs=win[:, k, 2048 + cbk * 512:2048 + (cbk + 1) * 512],
                                                                   start=(k == 0), stop=(k == 7)), r=['win', hTn], w=[pn_])
                                kb.op('act', lambda e: e.activation(out=vv[:, cbk * 512:(cbk + 1) * 512], in_=pt_[:], func=AF.Gelu_apprx_tanh),
                                      r=[pn_], w=['vv'])
                            kb.op('act', lambda e: e.activation(out=vn[:], in_=vv[:], func=AF.Square, accum_out=sv2[:, 0:1]), r=['vv'], w=['vn', 'sv2'])
                            kb.op('dve', lambda e: e.tensor_scalar(out=sv2[:, 1:2], in0=sv2[:, 0:1], scalar1=1.0 / 2048, scalar2=EPS,
                                                                   op0=ALU.mult, op1=ALU.add), r=['sv2'], w=['sv2'])
                            kb.op('act', lambda e: e.activation(out=sv2[:, 2:3], in_=sv2[:, 1:2], func=AF.Sqrt), r=['sv2'], w=['sv2'])
                            kb.op('dve', lambda e: e.reciprocal(out=sv2[:, 3:4], in_=sv2[:, 2:3]), r=['sv2'], w=['sv2'])
                            kb.op('dve', lambda e: e.scalar_tensor_tensor(out=vn[:], in0=vv[:], scalar=sv2[:, 3:4], in1=ngv[:],
                                                                          op0=ALU.mult, op1=ALU.mult), r=['vv', 'sv2', 'ngv'], w=['vn'])
                            for bq in range(4):
                                sq[0] += 1
                                sp_ = psp[sq[0] % 2]; spn = ('psp', sq[0] % 2)
                                for f4 in range(4):
                                    fc = bq * 4 + f4
                                    kb.op('pe', lambda e: e.matmul(sp_[:, f4, :], lhsT=vn[:, fc * 128:(fc + 1) * 128], rhs=wsT[:, fc // 2, :],
                                                                   start=True, stop=True), r=['vn', 'wsT'], w=[spn])
                                kb.op('dve', lambda e: e.tensor_tensor(out=svt[:], in0=sp_[:], in1=bsf[:, bq * 4:(bq + 1) * 4, :], op=ALU.add),
                                      r=[spn, 'bs16'], w=['svt'])
                                kb.op('dve', lambda e: e.tensor_tensor(out=zT[:, bq * 4:(bq + 1) * 4, q_ * 128:(q_ + 1) * 128], in0=svt[:],
                                                                       in1=uT[:, bq * 4:(bq + 1) * 4, q_ * 128:(q_ + 1) * 128], op=ALU.mult),
                                      r=['svt', 'uT'], w=['zT'])
                        kb.dma('sp', lambda e: e.dma_start(out=zbuf2[:, :, g * 512:(g + 1) * 512].rearrange("c p t -> p c t"), in_=zT[:]),
                               r=['zT'], w=['zbuf2'])
                    kb.barrier()
        with ExitStack() as es3:
            mods = load_mod_bc(es3, l, 1, norm1_g[l:l + 1, :], want='G')
            wo = kb.sb(es3, [128, 16, D], BF16, 'wo')
            for hh in range(2):
                kb.dma('pool', lambda e: e.dma_start(out=wo[:, hh * 8:(hh + 1) * 8, :],
                                                     in_=sg_w_out[j, hh * 1024:(hh + 1) * 1024, :].rearrange("(c p) f -> p c f", p=128)), w=['wo'])
            zt = [kb.sb(es3, [128, 16, 512], BF16, 'zt') for _ in range(2)]
            xb_ = [kb.sb(es3, [128, D], F32, 'xt') for _ in range(3)]
            tmp = [kb.sb(es3, [128, D], F32, 'tmp') for _ in range(2)]
            po = [kb.ps(es3, [128, 2, 512], F32, 'po') for _ in range(3)]
            for g in range(T // 512):
                zb = zt[g % 2]; zn = ('zt', g % 2)
                kb.dma('sp', lambda e: e.dma_start(out=zb[:], in_=zbuf2[:, :, g * 512:(g + 1) * 512].rearrange("c p t -> p c t")),
                       r=['zbuf2'], w=[zn])
                grp = 0 if g * 512 < TS else 1
                G_ = mods[grp][2]; gn = ('G', grp)
                for q_ in range(4):
                    i = g * 4 + q_
                    xb = xb_[i % 3]; xn = ('xt', i % 3)
                    load_x_tile(False, i, xb, xn)
                    pt_ = po[i % 3]; pn_ = ('po', i % 3)
                    for hf in range(2):
                        for c in range(16):
                            kb.op('pe', lambda e: e.matmul(pt_[:, hf, :], lhsT=zb[:, c, q_ * 128:(q_ + 1) * 128],
                                                           rhs=wo[:, c, hf * 512:(hf + 1) * 512], start=(c == 0), stop=(c == 15)),
                                  r=[zn, 'wo'], w=[pn_])
                    tm = tmp[i % 2]; tn = ('tmp', i % 2)
                    kb.op('dve', lambda e: e.tensor_tensor(out=tm[:], in0=pt_[:].rearrange("p a b -> p (a b)"), in1=G_[:], op=ALU.mult),
                          r=[pn_, gn], w=[tn])
                    kb.op('dve', lambda e: e.tensor_tensor(out=tm[:], in0=tm[:], in1=xb[:], op=ALU.add), r=[tn, xn], w=[tn])
                    kb.dma('sp', lambda e: e.dma_start(out=xres[i * 128:(i + 1) * 128, :], in_=tm[:]), r=[tn], w=['xres_w'])
            kb.barrier()

    NIT = 26

    def moe_phase(l):
        with ExitStack() as es:
            aff_tm = kb.sb(es, [128, NT, NE], F32, 'afftm')
            thr_bc = kb.sb(es, [128, 2, NE], F32, 'thrbc')
            with ExitStack() as es1:
                rt = kb.sb(es1, [128, 8, NE], BF16, 'rt')
                kb.dma('pool', lambda e: e.dma_start(out=rt[:], in_=moe_router[l].rearrange("(k p) e -> p k e", p=128)), w=['rt'])
                aff_em = kb.sb(es1, [NE, T], F32, 'affem')
                xb_ = [kb.sb(es1, [128, D], F32, 'xt') for _ in range(2)]
                scr = kb.sb(es1, [128, D], F32, 'scr')
                hnb = [kb.sb(es1, [128, D], BF16, 'hn') for _ in range(3)]
                stt = [kb.sb(es1, [128, 4], F32, 'st') for _ in range(2)]
                hT = [kb.sb(es1, [128, 8, 128], BF16, 'hT') for _ in range(2)]
                ex = kb.sb(es1, [128, NE], F32, 'ex')
                sm = kb.sb(es1, [128, 2], F32, 'sm')
                pT = [kb.ps(es1, [128, 8, 128], BF16, 'pT') for _ in range(2)]
                pr = [kb.ps(es1, [128, NE], F32, 'pr') for _ in range(2)]
                pa = [kb.ps(es1, [NE, 128], F32, 'pa') for _ in range(2)]
                for grp in range(2):
                    with ExitStack() as es2:
                        A_, B_, _g = load_mod_bc(es2, l, 2, norm2_g[l:l + 1, :], cnds=(grp,), want='AB')[grp]
                        an, bn = ('A', grp), ('B', grp)
                        i0, i1 = (0, TS // 128) if grp == 0 else (TS // 128, NT)
                        for i in range(i0, i1):
                            b = i % 2
                            hb = hnb[i % 3]; hbn = ('hn', i % 3)
                            load_x_tile(False, i, xb_[b], ('xt', b))
                            norm_mod(xb_[b], ('xt', b), A_, an, B_, bn, hb, hbn, scr, stt[b], ('st', b))
                            kb.dma('sp', lambda e: e.dma_start(out=hn2d[i * 128:(i + 1) * 128, :], in_=hb[:]), r=[hbn], w=['hn2d'])
                            for k in range(8):
                                kb.op('pe', lambda e: e.transpose(out=pT[b][:, k, :], in_=hb[:, k * 128:(k + 1) * 128],
                                                                  identity=ident_b[:]), r=[hbn, 'identb'], w=[('pT', b)])
                            kb.op('act', lambda e: e.copy(out=hT[b][:], in_=pT[b][:]), r=[('pT', b)], w=[('hT', b)])
                            for k in range(8):
                                kb.op('pe', lambda e: e.matmul(pr[b][:], lhsT=hT[b][:, k, :], rhs=rt[:, k, :], start=(k == 0), stop=(k == 7)),
                                      r=[('hT', b), 'rt'], w=[('pr', b)])
                            kb.op('act', lambda e: e.activation(out=ex[:], in_=pr[b][:], func=AF.Exp, accum_out=sm[:, 0:1]),
                                  r=[('pr', b)], w=['ex', 'sm'])
                            kb.op('dve', lambda e: e.reciprocal(out=sm[:, 1:2], in_=sm[:, 0:1]), r=['sm'], w=['sm'])
                            kb.op('dve', lambda e: e.tensor_scalar(out=aff_tm[:, i, :], in0=ex[:], scalar1=sm[:, 1:2], scalar2=None, op0=ALU.mult),
                                  r=['ex', 'sm'], w=['afftm'])
                            kb.op('pe', lambda e: e.transpose(out=pa[b][:], in_=aff_tm[:, i, :], identity=ident_f[:]),
                                  r=['afftm', 'identf'], w=[('pa', b)])
                            kb.op('act', lambda e: e.copy(out=aff_em[:, i * 128:(i + 1) * 128], in_=pa[b][:]), r=[('pa', b)], w=['affem'])
                        kb.barrier()
                kb.dma('pool', lambda e: e.dma_start(out=aff_loc_t.ap(), in_=aff_em[:]), r=['affem'], w=['affloc'])
                kb.cc(lambda e: e.collective_compute("AllGather", ALU.bypass, replica_groups=[list(range(NCORES))],
                                                     ins=[aff_loc_t.ap().opt()], outs=[aff_all_t.ap().opt()]), r=['affloc'], w=['affall'])
                kb.barrier()
            if MOE_STOP <= 1:
                return
            with ExitStack() as es1:
                AV = kb.sb(es1, [128, T], F32, 'AV')
                junk = kb.sb(es1, [128, TS], BF16, 'junk')
                lo = kb.sb(es1, [128, 2], F32, 'lo'); hi = kb.sb(es1, [128, 2], F32, 'hi'); mid = kb.sb(es1, [128, 2], F32, 'mid')
                cnt = kb.sb(es1, [128, 2], F32, 'cnt'); capt = kb.sb(es1, [128, 2], F32, 'capt'); ge = kb.sb(es1, [128, 2], F32, 'ge')
                dl = kb.sb(es1, [128, 2], F32, 'dl')
                cT = kb.sb(es1, [2, 128], F32, 'cT'); c16 = kb.sb(es1, [2, NE], F32, 'c16'); cB = kb.sb(es1, [2, NCORES, NE], F32, 'cB')
                tot = kb.sb(es1, [128, 2], F32, 'tot')
                p1 = kb.ps(es1, [2, 128], F32, 'p1'); p2 = kb.ps(es1, [128, 2], F32, 'p2')
                kb.dma('sp', lambda e: e.dma_start(out=AV[:], in_=aff_all_t.ap()), r=['affall'], w=['AV'])
                Gm = kb.sb(es1, [128, 128], F32, 'Gm'); dd_ = kb.sb(es1, [128, 128], F32, 'dd')
                kb.op('dve', lambda e: e.tensor_scalar(out=dd_[:], in0=iota_row[:, 0:128], scalar1=iota_col[:, 0:1], scalar2=None, op0=ALU.subtract),
                      r=['iotar', 'iotac'], w=['dd'])
                kb.op('dve', lambda e: e.memset(Gm[:], 0.0), w=['Gm'])
                for kk in range(-7, 8):
                    kb.op('dve', lambda e: e.scalar_tensor_tensor(out=Gm[:], in0=dd_[:], scalar=float(16 * kk), in1=Gm[:], op0=ALU.is_equal, op1=ALU.add),
                          r=['dd', 'Gm'], w=['Gm'])
                kb.op('dve', lambda e: e.memset(lo[:], 0.0), w=['lo'])
                kb.op('dve', lambda e: e.memset(hi[:], 1.0), w=['hi'])
                kb.op('dve', lambda e: e.memset(capt[:, 0:1], float(TS)), w=['capt'])
                kb.op('dve', lambda e: e.memset(capt[:, 1:2], float(TP)), w=['capt'])
                for it in range(NIT):
                    kb.op('dve', lambda e: e.tensor_tensor(out=mid[:], in0=lo[:], in1=hi[:], op=ALU.add), r=['lo', 'hi'], w=['mid'])
                    kb.op('dve', lambda e: e.tensor_scalar(out=mid[:], in0=mid[:], scalar1=0.5, scalar2=None, op0=ALU.mult), r=['mid'], w=['mid'])
                    kb.op('dve', lambda e: e.tensor_scalar(out=junk[:, 0:TS], in0=AV[:, 0:TS], scalar1=mid[:, 0:1], scalar2=0.0,
                                                           op0=ALU.is_ge, op1=ALU.add, accum_out=cnt[:, 0:1]), r=['AV', 'mid'], w=['junk', 'cnt'])
                    kb.op('dve', lambda e: e.tensor_scalar(out=junk[:, 0:TP], in0=AV[:, TS:T], scalar1=mid[:, 1:2], scalar2=0.0,
                                                           op0=ALU.is_ge, op1=ALU.add, accum_out=cnt[:, 1:2]), r=['AV', 'mid'], w=['junk', 'cnt'])
                    kb.op('pe', lambda e: e.matmul(p2[:], lhsT=Gm[:], rhs=cnt[:], start=True, stop=True), r=['Gm', 'cnt'], w=['p2'])
                    kb.op('dve', lambda e: e.tensor_tensor(out=ge[:], in0=p2[:], in1=capt[:], op=ALU.is_ge), r=['p2', 'capt'], w=['ge'])
                    kb.op('dve', lambda e: e.tensor_tensor(out=dl[:], in0=mid[:], in1=lo[:], op=ALU.subtract), r=['mid', 'lo'], w=['dl'])
                    kb.op('dve', lambda e: e.tensor_tensor(out=dl[:], in0=dl[:], in1=ge[:], op=ALU.mult), r=['dl', 'ge'], w=['dl'])
                    kb.op('dve', lambda e: e.tensor_tensor(out=lo[:], in0=lo[:], in1=dl[:], op=ALU.add), r=['lo', 'dl'], w=['lo'])
                    kb.op('dve', lambda e: e.tensor_tensor(out=dl[:], in0=hi[:], in1=mid[:], op=ALU.subtract), r=['mid', 'hi'], w=['dl'])
                    kb.op('dve', lambda e: e.tensor_tensor(out=dl[:], in0=dl[:], in1=ge[:], op=ALU.mult), r=['dl', 'ge'], w=['dl'])
                    kb.op('dve', lambda e: e.tensor_tensor(out=hi[:], in0=mid[:], in1=dl[:], op=ALU.add), r=['mid', 'dl'], w=['hi'])
                for cnd in range(2):
                    kb.dma('sp', lambda e: e.dma_start(out=thr_d[cnd].rearrange("(e a) -> e a", a=1), in_=lo[0:NE, cnd:cnd + 1],
                                                       allow_slow_non_contiguous=True), r=['lo'], w=['thrd'])
                for cnd in range(2):
                    kb.dma('sp', lambda e: e.dma_start(out=thr_bc[:, cnd, :], in_=thr_d[cnd:cnd + 1, :].partition_broadcast(128)),
                           r=['thrd'], w=['thrbc'])
                kb.barrier()
            if MOE_STOP <= 2:
                return
            with ExitStack() as es1:
                mods = load_mod_bc(es1, l, 2, norm2_g[l:l + 1, :], want='G')
                posm = kb.sb(es1, [128, NT, NE], F32, 'posm')
                gp = kb.sb(es1, [128, NT, NE, 3], BF16, 'gp')
                info = kb.sb(es1, [128, NT, 6], BF16, 'info')
                trash = kb.sb(es1, [128, 1], F32, 'trash')
                kb.op('dve', lambda e: e.tensor_scalar(out=trash[:], in0=iota_col[:], scalar1=float(T), scalar2=None, op0=ALU.add), r=['iotac'], w=['trash'])
                with ExitStack() as es2:
                    mk = kb.sb(es2, [128, NT, NE], BF16, 'mk')
                    base = kb.sb(es2, [128, NE], F32, 'base')
                    r1 = kb.sb(es2, [128, NT, NE], F32, 'r1'); r2 = kb.sb(es2, [128, NT, NE], F32, 'r2')
                    tidx = kb.sb(es2, [128, NT], F32, 'tidx')
                    pc = [kb.ps(es2, [128, 2, NE], F32, 'pc') for _ in range(2)]
                    for i in range(NT):
                        cnd = 0 if i < TS // 128 else 1
                        if i == 0 or i == TS // 128:
                            kb.op('dve', lambda e: e.memset(base[:], 0.0), w=['base'])
                        kb.op('dve', lambda e: e.tensor_tensor(out=mk[:, i, :], in0=aff_tm[:, i, :], in1=thr_bc[:, cnd, :], op=ALU.is_ge),
                              r=['afftm', 'thrbc'], w=[('mk', i)])
                        pcb = pc[i % 2]; pcn = ('pc', i % 2)
                        kb.op('pe', lambda e: e.matmul(pcb[:, 0, :], lhsT=tri_b[:], rhs=mk[:, i, :], start=True, stop=True), r=['trib', ('mk', i)], w=[pcn])
                        kb.op('pe', lambda e: e.matmul(pcb[:, 1, :], lhsT=ones_b[:], rhs=mk[:, i, :], start=True, stop=True), r=['onesb', ('mk', i)], w=[pcn])
                        kb.op('dve', lambda e: e.tensor_tensor(out=posm[:, i, :], in0=pcb[:, 0, :], in1=base[:], op=ALU.add), r=[pcn, 'base'], w=['posm'])
                        kb.op('dve', lambda e: e.tensor_tensor(out=posm[:, i, :], in0=posm[:, i, :], in1=mk[:, i, :], op=ALU.mult), r=['posm', ('mk', i)], w=['posm'])
                        kb.op('dve', lambda e: e.tensor_tensor(out=base[:], in0=base[:], in1=pcb[:, 1, :], op=ALU.add), r=[pcn, 'base'], w=['base'])
                    kb.op('dve', lambda e: e.tensor_copy(out=gp[:, :, :, 0], in_=aff_tm[:]), r=['afftm'], w=['gp'])
                    kb.op('dve', lambda e: e.tensor_tensor(out=r1[:], in0=aff_tm[:], in1=gp[:, :, :, 0], op=ALU.subtract), r=['afftm', 'gp'], w=['r1'])
                    kb.op('dve', lambda e: e.tensor_copy(out=gp[:, :, :, 1], in_=r1[:]), r=['r1'], w=['gp'])
                    kb.op('dve', lambda e: e.tensor_tensor(out=r2[:], in0=r1[:], in1=gp[:, :, :, 1], op=ALU.subtract), r=['r1', 'gp'], w=['r2'])
                    kb.op('dve', lambda e: e.tensor_copy(out=gp[:, :, :, 2], in_=r2[:]), r=['r2'], w=['gp'])
                    kb.op('pool', lambda e: e.iota(tidx[:], [[1, NT]], base=0, channel_multiplier=0, allow_small_or_imprecise_dtypes=True), w=['tidx'])
                    kb.op('dve', lambda e: e.tensor_copy(out=info[:, :, 0], in_=tidx[:]), r=['tidx'], w=['info'])
                    kb.op('dve', lambda e: e.tensor_copy(out=info[:, :, 1], in_=iota_col[:, 0:1].to_broadcast([128, NT])), r=['iotac'], w=['info'])
                    kb.op('dve', lambda e: e.memset(info[:, :, 2], 1.0), w=['info'])
                    kb.barrier()
                if MOE_STOP <= 3:
                    return
                Wt = [[kb.sb(es1, [128, 8, D], BF16, 'W%d' % m) for m in range(3)] for _ in range(2)]
                xsT = kb.sb(es1, [128, 8, CSL], BF16, 'xsT')
                hhT = kb.sb(es1, [128, 8, CSL], BF16, 'hhT')
                Sb = [kb.sb(es1, [128, CS], BF16, 'Sb') for _ in range(2)]
                xs = [kb.sb(es1, [128, D], BF16, 'xs') for _ in range(3)]
                ob = [kb.sb(es1, [128, D], F32, 'ob') for _ in range(2)]
                sgt = [kb.sb(es1, [128, 512], F32, 'sgt') for _ in range(2)]
                sinf = kb.sb(es1, [128, NSL, 6], F32, 'sinf')
                idxf = kb.sb(es1, [128, NSL], F32, 'idxf'); idxi = [kb.sb(es1, [128, NSL], I32, 'idxi') for _ in range(2)]
                gsl = [kb.sb(es1, [128, NSL], F32, 'gsl') for _ in range(2)]
                vtr = kb.sb(es1, [128, NSL], F32, 'vtr')
                pT = kb.ps(es1, [128, 8, 128], BF16, 'pT')
                pgu = [kb.ps(es1, [128, 2, 512], F32, 'pgu') for _ in range(2)]
                pdn = kb.ps(es1, [128, 2, 512], F32, 'pdn')
                psi = kb.ps(es1, [128, NSL, 6], F32, 'psi')
                wsrc = (moe_w_gate, moe_w_up, moe_w_down)

                def load_w(ex_):
                    for m in range(3):
                        kb.dma('pool', lambda e: e.dma_start(out=Wt[ex_ % 2][m][:], in_=wsrc[m][l, ex_].rearrange("(k p) f -> p k f", p=128)),
                               w=[('W', ex_ % 2, m)])
                load_w(0)
                sq = [0]; gq = [0]
                for ex_ in range(NE):
                    if ex_ + 1 < NE:
                        load_w(ex_ + 1)
                    Wg, Wu, Wd = Wt[ex_ % 2]
                    wn = [('W', ex_ % 2, m) for m in range(3)]
                    eb = ex_ % 2
                    kb.op('dve', lambda e: e.tensor_copy(out=info[:, :, 3:6], in_=gp[:, :, ex_, :]), r=['gp'], w=['info'])
                    for i in range(NT):
                        samp = i < TS // 128
                        ncol = min(CS, 128 * (i + 1)) if samp else min(CP, 128 * (i - TS // 128 + 1))
                        s0 = 0 if samp else CS // 128
                        sq[0] += 1
                        S_ = Sb[sq[0] % 2]; Sn = ('Sb', sq[0] % 2)
                        kb.op('dve', lambda e: e.tensor_scalar(out=S_[:, 0:ncol], in0=iota_row1[:, 0:ncol], scalar1=posm[:, i, ex_:ex_ + 1],
                                                               scalar2=None, op0=ALU.is_equal), r=['iotar1', 'posm'], w=[Sn])
                        last = (TS // 128 - 1) if samp else (NT - 1)
                        first_i = 0 if samp else TS // 128
                        for s_ in range(ncol // 128):
                            kb.op('pe', lambda e: e.matmul(psi[:, s0 + s_, :], lhsT=S_[:, s_ * 128:(s_ + 1) * 128], rhs=info[:, i, :],
                                                           start=(i == 0 and s_ == 0), stop=(i == NT - 1 and s_ == ncol // 128 - 1), skip_group_check=True), r=[Sn, 'info'], w=['psi'])
                    kb.op('dve', lambda e: e.tensor_copy(out=sinf[:], in_=psi[:]), r=['psi'], w=['sinf'])
                    kb.op('dve', lambda e: e.scalar_tensor_tensor(out=idxf[:], in0=sinf[:, :, 0], scalar=128.0, in1=sinf[:, :, 1], op0=ALU.mult, op1=ALU.add),
                          r=['sinf'], w=['idxf'])
                    kb.op('dve', lambda e: e.tensor_scalar(out=vtr[:], in0=sinf[:, :, 2], scalar1=-1.0, scalar2=1.0, op0=ALU.mult, op1=ALU.add), r=['sinf'], w=['vtr'])
                    kb.op('dve', lambda e: e.scalar_tensor_tensor(out=idxf[:], in0=vtr[:], scalar=trash[:, 0:1], in1=idxf[:], op0=ALU.mult, op1=ALU.add),
                          r=['vtr', 'trash', 'idxf'], w=['idxf'])
                    kb.op('dve', lambda e: e.tensor_copy(out=idxi[eb][:], in_=idxf[:]), r=['idxf'], w=[('idxi', eb)])
                    kb.op('dve', lambda e: e.tensor_tensor(out=gsl[eb][:], in0=sinf[:, :, 3], in1=sinf[:, :, 4], op=ALU.add), r=['sinf'], w=[('gsl', eb)])
                    kb.op('dve', lambda e: e.tensor_tensor(out=gsl[eb][:], in0=gsl[eb][:], in1=sinf[:, :, 5], op=ALU.add), r=['sinf', ('gsl', eb)], w=[('gsl', eb)])
                    for s_ in range(NSL):
                        gq[0] += 1
                        xb = xs[gq[0] % 3]; xn = ('xs', gq[0] % 3)
                        kb.dma('pool', lambda e: e.indirect_dma_start(out=xb[:], out_offset=None, in_=hn2d[:, :],
                                                                      in_offset=bass.IndirectOffsetOnAxis(ap=idxi[eb][:, s_:s_ + 1], axis=0)),
                               r=[('idxi', eb), 'hn2d'], w=[xn])
                        for k in range(8):
                            kb.op('pe', lambda e: e.transpose(out=pT[:, k, :], in_=xb[:, k * 128:(k + 1) * 128], identity=ident_b[:]),
                                  r=[xn, 'identb'], w=['pT'])
                        kb.op('act', lambda e: e.copy(out=xsT[:, :, s_ * 128:(s_ + 1) * 128], in_=pT[:]), r=['pT'], w=['xsT'])
                    cq = 0
                    for fc in range(8):
                        for (c0, c1) in ((0, 512), (512, 1024), (1024, CSL)):
                            cq += 1
                            pg = pgu[cq % 2]; pgn = ('pgu', cq % 2)
                            wdt = c1 - c0
                            for m, Wm in ((0, Wg), (1, Wu)):
                                for k in range(8):
                                    kb.op('pe', lambda e: e.matmul(pg[:, m, 0:wdt], lhsT=Wm[:, k, fc * 128:(fc + 1) * 128], rhs=xsT[:, k, c0:c1],
                                                                   start=(k == 0), stop=(k == 7)), r=[wn[m], 'xsT'], w=[pgn])
                            sg_ = sgt[cq % 2]; sgn = ('sgt', cq % 2)
                            kb.op('act', lambda e: e.activation(out=sg_[:, 0:wdt], in_=pg[:, 0, 0:wdt], func=AF.Silu), r=[pgn], w=[sgn])
                            kb.op('dve', lambda e: e.tensor_tensor(out=hhT[:, fc, c0:c1], in0=sg_[:, 0:wdt], in1=pg[:, 1, 0:wdt], op=ALU.mult),
                                  r=[sgn, pgn], w=['hhT'])
                    for s_ in range(NSL):
                        for hf in range(2):
                            for fc in range(8):
                                kb.op('pe', lambda e: e.matmul(pdn[:, hf, :], lhsT=hhT[:, fc, s_ * 128:(s_ + 1) * 128], rhs=Wd[:, fc, hf * 512:(hf + 1) * 512],
                                                               start=(fc == 0), stop=(fc == 7)), r=['hhT', wn[2]], w=['pdn'])
                        cnd = 0 if s_ < CS // 128 else 1
                        o_ = ob[s_ % 2]; on = ('ob', s_ % 2)
                        kb.op('dve', lambda e: e.scalar_tensor_tensor(out=o_[:], in0=pdn[:].rearrange("p a b -> p (a b)"), scalar=gsl[eb][:, s_:s_ + 1],
                                                                      in1=mods[cnd][2][:], op0=ALU.mult, op1=ALU.mult),
                              r=['pdn', ('gsl', eb), ('G', cnd)], w=[on])
                        kb.dma('pool', lambda e: e.indirect_dma_start(out=xres[:, :], out_offset=bass.IndirectOffsetOnAxis(ap=idxi[eb][:, s_:s_ + 1], axis=0),
                                                                      in_=o_[:], in_offset=None, compute_op=ALU.add),
                               r=[on, ('idxi', eb)], w=['xres_acc'])
                kb.barrier()

    def final_phase():
        with ExitStack() as es:
            gt = kb.sb(es, [128, D], F32, 'fg')
            kb.dma('sp', lambda e: e.dma_start(out=gt[:], in_=final_norm_g.partition_broadcast(128)), w=['fg'])
            xb_ = [kb.sb(es, [128, D], F32, 'xt') for _ in range(3)]
            scr = kb.sb(es, [128, D], F32, 'scr')
            ob = [kb.sb(es, [128, D], F32, 'ob') for _ in range(2)]
            stt = [kb.sb(es, [128, 4], F32, 'st') for _ in range(2)]
            for i in range(NT):
                xt = xb_[i % 3]; xn = ('xt', i % 3); st = stt[i % 2]; sn = ('st', i % 2)
                kb.dma('sp', lambda e: e.dma_start(out=xt[:], in_=xres[i * 128:(i + 1) * 128, :]), w=[xn])
                kb.op('act', lambda e: e.activation(out=scr[:], in_=xt[:], func=AF.Square, accum_out=st[:, 0:1]), r=[xn], w=['scr', sn])
                kb.op('dve', lambda e: e.tensor_scalar(out=st[:, 1:2], in0=st[:, 0:1], scalar1=1.0 / D, scalar2=EPS,
                                                       op0=ALU.mult, op1=ALU.add), r=[sn], w=[sn])
                kb.op('act', lambda e: e.activation(out=st[:, 2:3], in_=st[:, 1:2], func=AF.Sqrt), r=[sn], w=[sn])
                kb.op('dve', lambda e: e.reciprocal(out=st[:, 3:4], in_=st[:, 2:3]), r=[sn], w=[sn])
                o = ob[i % 2]; on = ('ob', i % 2)
                kb.op('dve', lambda e: e.scalar_tensor_tensor(out=o[:], in0=xt[:], scalar=st[:, 3:4], in1=gt[:],
                                                              op0=ALU.mult, op1=ALU.mult), r=[xn, sn, 'fg'], w=[on])
                kb.dma('sp', lambda e: e.dma_start(out=y_out[i * 128:(i + 1) * 128, :], in_=o[:]), r=[on], w=[('y', i)])
            kb.barrier()

    for l in range(n_layers):
        if l % 2 == 0:
            lru_phase(l, l // 2)
        else:
            sgu_phase(l, l // 2)
        if stage == 'mix' and l == n_layers - 1:
            break
        if not SKIP_MOE:
            moe_phase(l)
    final_phase()
    kb.finish()
    top.close()
    return nc


_PROG = {}


def _pos_embed():
    rows = TS // 64
    row = np.repeat(np.arange(rows, dtype=np.float32), 64)
    col = np.tile(np.arange(64, dtype=np.float32), rows)
    q = D // 4
    freq = np.exp(-np.log(10000.0) * np.arange(q, dtype=np.float32) / q).astype(np.float32)
    ar = row[:, None] * freq
    ac = col[:, None] * freq
    return np.concatenate([np.sin(ar), np.cos(ar), np.sin(ac), np.cos(ac)], axis=-1).astype(np.float32)


def make_in_maps(inp):
    f = lambda a: np.ascontiguousarray(np.asarray(a, dtype=np.float32))
    shared = {k: f(inp[k]) for k in ["norm1_g", "norm2_g", "w_mod", "b_mod", "lru_w_in", "lru_conv_w", "lru_conv_b",
                                     "lru_w_a", "lru_b_a", "lru_w_x", "lru_b_x", "lru_lam", "lru_w_out", "sg_w_in",
                                     "sg_norm_g", "sg_b_s", "sg_w_out", "moe_router", "moe_w_gate", "moe_w_up", "moe_w_down"]}
    shared["sg_w_sT"] = f(np.transpose(np.asarray(inp["sg_w_s"]), (0, 1, 3, 2)))
    shared["final_norm_g"] = f(inp["final_norm_g"]).reshape(1, D)
    shared["pos_in"] = _pos_embed()
    xs = f(inp["x_sample"]); xp = f(inp["x_prompt"]); st = f(inp["state_lru"]); c = f(inp["c"]); cc = f(inp["c_ctx"])
    maps = []
    for r in range(NCORES):
        m = dict(shared)
        m["x_in"] = np.concatenate([xs[r], xp[4 * r:4 * r + 4].reshape(TP, D)], axis=0)
        m["cond_in"] = np.stack([c[r], cc], axis=0)
        m["h0_in"] = st[r]
        maps.append(m)
    return maps


def kernel(**inp):
    if 'nc' not in _PROG:
        _PROG['nc'] = build_program()
    nc = _PROG['nc']
    maps = make_in_maps(inp)
    res = run_bass_kernel_spmd(nc, maps, core_ids=list(range(NCORES)))
    ys = np.stack([r["y_out"][:TS] for r in res.results], axis=0)
    yp = np.concatenate([r["y_out"][TS:].reshape(4, 256, D) for r in res.results], axis=0)
    ns = np.concatenate([r["st_out"] for r in res.results], axis=0)
    return (yp.astype(np.float32), ys.astype(np.float32), ns.astype(np.float32))
```
